# Optimizing a Trainium2 kernel written in Bass

```python
import jax, jax.numpy as jnp
from jax import lax
import numpy as np

D_MODEL = 1024
BATCH = 2
SEQ = 16384
DEPTH = 2
DEC_BATCH = 8
DEC_SEQ = 4096
PAST_LEN = 128

CHUNK = 128
A_HEADS = 8
A_HEAD_DIM = D_MODEL // A_HEADS
A_WIDTH = A_HEADS * A_HEAD_DIM
B_GROUPS = 4
B_WIDTH = D_MODEL
B_GROUP_DIM = B_WIDTH // B_GROUPS
N_BRANCH = 2
D_FF = ((8 * D_MODEL // 3 + 255) // 256) * 256
IN_WIDTH = 2 * A_WIDTH + B_WIDTH + N_BRANCH * D_MODEL
EPS = 1e-6

kernel_name = "hybrid_gmlp_fnet_macaron_encoder"


def rms_norm(x, g):
    x32 = x.astype(jnp.float32)
    y = x32 * lax.rsqrt(jnp.mean(x32 * x32, axis=-1, keepdims=True) + EPS)
    return (y * g.astype(jnp.float32)).astype(x.dtype)


def swiglu_ffn(h, w_gate, w_up, w_down):
    return (jax.nn.silu(h @ w_gate) * (h @ w_up)) @ w_down


def spatial_gating(u, v, g_v, w_s, b_s):
    bsz, s, _ = u.shape
    v = rms_norm(v, g_v).reshape(bsz, s // CHUNK, CHUNK, A_HEADS, A_HEAD_DIM)
    v = jnp.einsum('hpq,bcqhd->bcphd', w_s, v) + b_s.T[None, None, :, :, None]
    return u * v.reshape(bsz, s, A_WIDTH)


def fourier_mix(z):
    bsz, s, _ = z.shape
    zg = z.astype(jnp.float32).reshape(bsz, s, B_GROUPS, B_GROUP_DIM)
    f = jnp.fft.fft2(zg, axes=(1, 3), norm="ortho").real
    return f.reshape(bsz, s, B_WIDTH).astype(z.dtype)


def hybrid_mixer(h, w_in, g_v, w_s, b_s, w_branch_a, w_branch_b, w_out):
    z = h @ w_in
    splits = [A_WIDTH, 2 * A_WIDTH, 2 * A_WIDTH + B_WIDTH, 2 * A_WIDTH + B_WIDTH + D_MODEL]
    u, v, z_b, g_a, g_b = jnp.split(z, splits, axis=-1)
    y_a = spatial_gating(jax.nn.gelu(u, approximate=False), jax.nn.gelu(v, approximate=False),
                         g_v, w_s, b_s) @ w_branch_a
    y_b = fourier_mix(z_b) @ w_branch_b
    merged = jax.nn.sigmoid(g_a) * y_a + jax.nn.sigmoid(g_b) * y_b
    return merged @ w_out


def setup_inputs(seed: int = 0) -> dict:
    key = jax.random.key(seed)
    ks = jax.random.split(key, 24)

    def nrm(k, shape, scale):
        return jax.random.normal(k, shape, jnp.float32) * scale

    def gain(k, shape):
        return 1.0 + nrm(k, shape, 0.02)

    return {
        "x_prompt": nrm(ks[0], (BATCH, SEQ, D_MODEL), 1.0),
        "x_sample": nrm(ks[1], (DEC_BATCH, DEC_SEQ, D_MODEL), 1.0),
        "ffn1_norm": gain(ks[2], (DEPTH, D_MODEL)),
        "ffn1_w_gate": nrm(ks[3], (DEPTH, D_MODEL, D_FF), D_MODEL ** -0.5),
        "ffn1_w_up": nrm(ks[4], (DEPTH, D_MODEL, D_FF), D_MODEL ** -0.5),
        "ffn1_w_down": nrm(ks[5], (DEPTH, D_FF, D_MODEL), D_FF ** -0.5),
        "mix_norm": gain(ks[6], (DEPTH, D_MODEL)),
        "w_in": nrm(ks[7], (DEPTH, D_MODEL, IN_WIDTH), D_MODEL ** -0.5),
        "sgu_norm": gain(ks[8], (DEPTH, A_WIDTH)),
        "sgu_w": nrm(ks[9], (DEPTH, A_HEADS, CHUNK, CHUNK), CHUNK ** -0.5),
        "sgu_b": 1.0 + nrm(ks[10], (DEPTH, A_HEADS, CHUNK), 0.1),
        "w_branch_a": nrm(ks[11], (DEPTH, A_WIDTH, D_MODEL), A_WIDTH ** -0.5),
        "w_branch_b": nrm(ks[12], (DEPTH, B_WIDTH, D_MODEL), B_WIDTH ** -0.5),
        "w_out": nrm(ks[13], (DEPTH, D_MODEL, D_MODEL), D_MODEL ** -0.5),
        "ffn2_norm": gain(ks[14], (DEPTH, D_MODEL)),
        "ffn2_w_gate": nrm(ks[15], (DEPTH, D_MODEL, D_FF), D_MODEL ** -0.5),
        "ffn2_w_up": nrm(ks[16], (DEPTH, D_MODEL, D_FF), D_MODEL ** -0.5),
        "ffn2_w_down": nrm(ks[17], (DEPTH, D_FF, D_MODEL), D_FF ** -0.5),
        "final_norm": gain(ks[18], (D_MODEL,)),
    }


def reference(x_prompt, x_sample, ffn1_norm, ffn1_w_gate, ffn1_w_up, ffn1_w_down,
              mix_norm, w_in, sgu_norm, sgu_w, sgu_b, w_branch_a, w_branch_b, w_out,
              ffn2_norm, ffn2_w_gate, ffn2_w_up, ffn2_w_down, final_norm):
    def trunk(x):
        for l in range(DEPTH):
            x = x + 0.5 * swiglu_ffn(rms_norm(x, ffn1_norm[l]),
                                     ffn1_w_gate[l], ffn1_w_up[l], ffn1_w_down[l])
            x = x + hybrid_mixer(rms_norm(x, mix_norm[l]), w_in[l], sgu_norm[l],
                                 sgu_w[l], sgu_b[l], w_branch_a[l], w_branch_b[l], w_out[l])
            x = x + 0.5 * swiglu_ffn(rms_norm(x, ffn2_norm[l]),
                                     ffn2_w_gate[l], ffn2_w_up[l], ffn2_w_down[l])
        return rms_norm(x, final_norm)

    y_prompt = trunk(x_prompt)
    y_sample = trunk(x_sample)
    return (y_prompt, y_sample)
```

```python
import numpy as np
import concourse.bass as bass
import concourse.mybir as mybir
from concourse.bass_utils import run_bass_kernel_spmd

F32 = mybir.dt.float32
BF16 = mybir.dt.bfloat16
AF = mybir.ActivationFunctionType
ALU = mybir.AluOpType

D = 1024
DFF = 2816
L = 2
NCORE = 8
TOK = 8192
G = 1024
NGRP = TOK // G
NB = 8
EPS = 1e-6
SP_TOK = 2048
S_PROMPT = 16384
S_SAMPLE = 4096
FFN_SPLITS = [(0, 6), (6, 14), (14, 22)]
RING = 4
NHB = 6
SLOT_BYTES = 8192
FFN1_BLOCKS = [(0, 256), (256, 768), (768, 1536), (1536, 2304), (2304, 2816)]


class Buf:
    __slots__ = ("name", "w", "r", "multi")

    def __init__(self, name, multi=False):
        self.name = name
        self.multi = multi
        self.w = {} if multi else None
        self.r = {}


class Chan:
    def __init__(self, sem):
        self.sem = sem
        self.count = 0


class Sched:
    COMPUTE = ("pe", "act", "dve", "pool")

    def __init__(self, nc, sem_alloc, dry=False):
        self.nc = nc
        self.dry = dry
        self.q = {"pe": [], "act": [], "dve": [], "pool": [], "sp": []}
        self.esem = {}
        self.ecount = {}
        self.sem_alloc = sem_alloc
        self.chans = []
        if not dry:
            for e in self.COMPUTE:
                self.esem[e] = sem_alloc("eng_" + e)
                self.ecount[e] = 0
        self.barrier_tokens = {q: {} for q in self.q}

    def chan(self, name):
        if self.dry:
            return None
        c = Chan(self.sem_alloc("ch_" + name))
        self.chans.append(c)
        return c

    def _deps(self, queue, reads, writes):
        deps = {}

        def add(tok):
            if tok is None:
                return
            s, v = tok
            if deps.get(s, (None, 0))[1] < v:
                deps[s] = (s, v)

        for b in reads:
            if b.multi:
                for t in b.w.values():
                    add(t)
            else:
                add(b.w)
        for b in writes:
            if not b.multi:
                add(b.w)
            for t in b.r.values():
                add(t)
        bt = self.barrier_tokens[queue]
        if bt:
            for t in bt.values():
                add(t)
            self.barrier_tokens[queue] = {}
        return list(deps.values())

    def _commit(self, tok, reads, writes):
        for b in reads:
            b.r[id(tok[0])] = tok
        for b in writes:
            if b.multi:
                b.w[id(tok[0])] = tok
            else:
                b.w = tok
                b.r = {}

    def op(self, eng, fn, reads=(), writes=()):
        if self.dry:
            return None
        deps = self._deps(eng, reads, writes)
        self.ecount[eng] += 1
        tok = (self.esem[eng], self.ecount[eng])
        self.q[eng].append((fn, deps, (self.esem[eng], 1)))
        self._commit(tok, reads, writes)
        return tok

    def dma(self, queue, fn, chan, reads=(), writes=(), inc=16):
        if self.dry:
            return None
        deps = self._deps(queue, reads, writes)
        chan.count += inc
        tok = (chan.sem, chan.count)
        self.q[queue].append((fn, deps, (chan.sem, inc)))
        self._commit(tok, reads, writes)
        return tok

    def barrier(self):
        if self.dry:
            return
        toks = {}
        for e in self.COMPUTE:
            if self.ecount[e]:
                toks[id(self.esem[e])] = (self.esem[e], self.ecount[e])
        for c in self.chans:
            if c.count:
                toks[id(c.sem)] = (c.sem, c.count)
        for q in self.q:
            d = dict(self.barrier_tokens[q])
            d.update(toks)
            self.barrier_tokens[q] = d

    def final_tokens(self):
        toks = []
        for e in self.COMPUTE:
            if self.ecount[e]:
                toks.append((self.esem[e], self.ecount[e]))
        for c in self.chans:
            if c.count:
                toks.append((c.sem, c.count))
        return toks

    def emit(self, queue, eng):
        known = {}
        own = self.esem.get(queue)
        for fn, deps, inc in self.q[queue]:
            for s, v in deps:
                if queue == "pe" and s is own:
                    continue
                if known.get(id(s), 0) < v:
                    eng.wait_ge(s, v)
                    known[id(s)] = v
            ins = fn()
            ins.then_inc(inc[0], inc[1])
        return known


def _tables(rank):
    t = np.arange(128, dtype=np.float64)
    ang = 2 * np.pi * np.outer(t, t) / 128.0
    d1p = np.concatenate([np.cos(ang), -np.sin(ang)], axis=1)
    d1s = np.zeros((128, 256))
    t32 = np.arange(32, dtype=np.float64)
    a32 = 2 * np.pi * np.outer(t32, t32) / 32.0
    for q in range(4):
        d1s[q * 32:(q + 1) * 32, q * 32:(q + 1) * 32] = np.cos(a32)
        d1s[q * 32:(q + 1) * 32, 128 + q * 32:128 + (q + 1) * 32] = -np.sin(a32)
    tlo = t[:, None, None]
    j = t[None, :, None]
    rp = np.arange(16, dtype=np.float64)[None, None, :]
    k = j + 128.0 * (16 * rank + rp)
    th = 2 * np.pi * ((tlo * k) % S_PROMPT) / S_PROMPT
    c = np.cos(th) / np.sqrt(S_PROMPT)
    s = np.sin(th) / np.sqrt(S_PROMPT)
    tp = np.concatenate([s, c, -s], axis=2).reshape(128, 128 * 48)
    kl = np.arange(32, dtype=np.float64)[None, :, None]
    r = t[None, None, :]
    k = kl + 32.0 * r
    th = 2 * np.pi * ((tlo * k) % S_SAMPLE) / S_SAMPLE
    c = np.cos(th) / np.sqrt(S_SAMPLE)
    s = np.sin(th) / np.sqrt(S_SAMPLE)
    ts = np.concatenate([s, c, -s], axis=2).reshape(128, 32 * 384)
    m = (np.arange(2)[None, :, None] * 128 + t[:, None, None])
    cc = np.arange(256, dtype=np.float64)[None, None, :]
    a = 2 * np.pi * ((m * cc) % 256) / 256.0
    cdt = np.stack([np.cos(a) / 16.0, np.sin(a) / 16.0], axis=2).reshape(128, 2 * 2 * 256)
    f = np.float32
    return dict(d1p=d1p.astype(f), d1s=d1s.astype(f), tp=tp.astype(f), ts=ts.astype(f), cdt=cdt.astype(f))


class Builder:
    def __init__(self, debug=False, stop_after=None):
        self.debug = debug
        self.stop_after = stop_after
        nc = self.nc = bass.Bass("TRN2", target_bir_lowering=False)
        dk = "ExternalOutput" if debug else "Internal"

        def din(name, shape, dt=F32):
            return nc.dram_tensor(name, list(shape), dt, kind="ExternalInput").ap()

        def dscr(name, shape, dt, kind="Internal"):
            return nc.dram_tensor(name, list(shape), dt, kind=kind).ap()

        self.x_in = din("x", [TOK, D])
        self.w_src = {}
        for pre in ("ffn1", "ffn2"):
            self.w_src[pre + "_g"] = din(pre + "_w_gate", [L, D, DFF])
            self.w_src[pre + "_u"] = din(pre + "_w_up", [L, D, DFF])
            self.w_src[pre + "_d"] = din(pre + "_w_down", [L, DFF, D])
        self.w_src["win"] = din("w_in", [L, D, 5 * D])
        self.w_src["wa"] = din("w_branch_a", [L, D, D])
        self.w_src["wb"] = din("w_branch_b", [L, D, D])
        self.w_src["wo"] = din("w_out", [L, D, D])
        self.gains = din("gains", [L, 3, 128, D])
        self.gfinal = din("gfinal", [128, D])
        self.gvb = din("gvb", [L, 128, D])
        self.wsT = din("wsT", [L, 128, D])
        self.bsrow = din("bsrow", [L, 1, D])
        self.ident_in = din("ident", [128, 128])
        self.ones_in = din("ones", [1, 128])
        self.d1p_in = din("d1p", [128, 256])
        self.d1s_in = din("d1s", [128, 256])
        self.tp_in = din("tp", [128, 128 * 48])
        self.ts_in = din("ts", [128, 32 * 384])
        self.cdt_in = din("cdt", [128, 1024])
        self.y_out = nc.dram_tensor("y", [TOK, D], F32, kind="ExternalOutput").ap()

        self.w_bf = {}
        for k, src in self.w_src.items():
            self.w_bf[k] = dscr("bf_" + k, src.shape, BF16)
        self.wbp = dscr("bf_wbp", [L, 2 * D, D], BF16)
        self.x1d = dscr("x1d", [TOK, D], F32, kind=dk)
        self.t1d = dscr("t1d", [8, 128, TOK], BF16, kind=dk)
        self.sgbd = dscr("sgbd", [8, 128, TOK], BF16, kind=dk)
        self.fd = dscr("fd", [2, 8, 128, TOK], BF16, kind=dk)
        self.zs = dscr("zs", [8, S_SAMPLE, 128], BF16, kind=dk)
        self.zp_in_t = nc.dram_tensor("zp_in", [8 * 2 * SP_TOK, 128], BF16)
        self.zp_out_t = nc.dram_tensor("zp_out", [NCORE * 8 * 2 * SP_TOK, 128], BF16)
        self.zp_in = self.zp_in_t.ap()
        self.zp_out = self.zp_out_t.ap()

        self.arena_bytes = 207 * 1024
        self.arena = nc.alloc_sbuf_tensor("arena", [128, self.arena_bytes // 2], BF16)
        self.psum = [nc.alloc_psum_tensor("ps%d" % i, [128, 512], F32) for i in range(6)]
        self.psum_bf = [nc.alloc_psum_tensor("psb%d" % i, [128, 1024], BF16) for i in range(2)]

        self._free_sems = []
        self.sem_count = 0
        self.cc_sem = nc.alloc_semaphore("cc_done")

    def sem(self, name):
        self.sem_count += 1
        return self.nc.alloc_semaphore(name)

    def view(self, off, nbytes, dt=BF16, pat=None, **kw):
        ap = self.arena[:, off // 2:(off + nbytes) // 2]
        if dt != BF16:
            ap = ap.bitcast(dt)
        if pat is not None:
            ap = ap.rearrange(pat, **kw)
        return ap

    def build(self):
        nc = self.nc
        self.S = Sched(nc, None, dry=True)
        self.pieces = []
        self.piece_idx = 0
        self.dry = True
        self.program()
        n_pieces = len(self.pieces)
        self.S = Sched(nc, self.sem, dry=False)
        self.dry = False
        self.piece_list = self.pieces
        self.pieces = []
        self.piece_idx = 0
        self.program()
        assert len(self.pieces) == n_pieces
        S = self.S
        fin = S.final_tokens()
        with nc.Block() as block:
            @block.tensor
            def _(e):
                S.emit("pe", nc.tensor)

            @block.scalar
            def _(e):
                S.emit("act", nc.scalar)

            @block.vector
            def _(e):
                S.emit("dve", nc.vector)

            @block.gpsimd
            def _(e):
                S.emit("pool", nc.gpsimd)

            @block.sync
            def _(e):
                known = S.emit("sp", nc.sync)
                for s, v in fin:
                    if known.get(id(s), 0) < v:
                        nc.sync.wait_ge(s, v)
        return nc

    def setup_map(self):
        v = self.view
        self.X = [v(b * 4096, 4096, F32) for b in range(NB)]
        self.HT = v(32768, 16384, BF16, "p (c t) -> p c t", c=8)
        self.MT = self.HT
        self.ACTT = v(49152, 16384, BF16, "p (c t) -> p c t", c=8)
        self.UT = v(49152, 16384, BF16, "p (c t) -> p c t", c=8)
        self.VN = [v(65536 + b * 2048, 2048, BF16) for b in range(NB)]
        self.YO = [v(65536 + i * 4096, 4096, F32) for i in range(2)]
        self.FT = v(81920, 32768, BF16, "p (c t) -> p c t", c=16)
        self.SLOT = [v(114688 + i * SLOT_BYTES, SLOT_BYTES, BF16) for i in range(RING)]
        stg = 147456
        self.TSIN = [v(stg + i * 8192, 8192, BF16, "p (a c t) -> p a c t", a=2, c=2) for i in range(2)]
        self.SGA = v(stg, 8192, BF16, "p (c t) -> p c t", c=4)
        self.T1S = [v(stg + 8192 + i * 1024, 1024, BF16) for i in range(4)]
        self.SGBS = [v(stg + 12288 + i * 1024, 1024, BF16) for i in range(4)]
        self.ZBS = [v(stg + 16384 + i * 1024, 1024, BF16) for i in range(4)]
        m = 167936
        self.GN = [v(m + i * 4096, 4096, F32) for i in range(3)]
        m += 12288
        self.HB = [v(m + i * 2048, 2048, BF16) for i in range(NHB)]
        m += 2048 * NHB
        self.SG = [v(m + i * 1024, 1024, BF16) for i in range(2)]
        m += 2048
        self.JUNK = v(m, 2048, BF16)
        m += 2048
        self.WST = v(m, 2048, BF16)
        m += 2048
        self.BSR = v(m, 2048, BF16)
        m += 2048
        self.ONES = v(m, 256, BF16)
        m += 256
        self.IDENT = v(m, 256, BF16)
        m += 256
        self.GVB = v(m, 4096, F32)
        m += 4096
        self.SS = v(m, 64, F32)
        m += 64
        self.RSTD = v(m, 64, F32)
        m += 64
        self.D1P = v(m, 512, BF16)
        m += 512
        self.D1S = v(m, 512, BF16)
        m += 512
        self.TMPF = [v(m + i * 2048, 2048, F32) for i in range(2)]
        m += 4096
        assert m <= self.arena_bytes, m
        self.ZT = v(0, 32768, BF16, "p (t c) -> p t c", t=128)
        self.Y = v(32768, 65536, BF16, "p (c j) -> p c j", c=128)
        self.FST = [v(98304 + i * 16384, 16384, BF16) for i in range(2)]
        self.TPT = v(131072, 12288, BF16, "p (j c) -> p j c", j=128)
        self.TST = v(143360, 24576, BF16, "p (j c) -> p j c", j=32)
        self.WBS = v(0, 16384, BF16, "p (c n) -> p c n", c=8)
        self.CDT = v(16384, 2048, BF16, "p (m t c) -> p m t c", m=2, t=2)
        self.WPS = [v(20480 + i * 2048, 2048, BF16) for i in range(2)]

    def program(self):
        S = self.S
        nc = self.nc
        self.setup_map()
        self.bank_i = 0
        self.bankbf_i = 0
        self.pbuf = [Buf("ps%d" % i) for i in range(6)]
        self.pbbuf = [Buf("psb%d" % i) for i in range(2)]
        self.bX = [Buf("X%d" % b) for b in range(NB)]
        self.bHT = [Buf("HT%d" % b) for b in range(NB)]
        self.bUT = [[Buf("UT%d_%d" % (c, st)) for st in range(2)] for c in range(8)]
        self.bVN = [Buf("VN%d" % b) for b in range(NB)]
        self.bFT = Buf("FT", multi=True)
        self.bSLOT = [Buf("SLOT%d" % i, multi=True) for i in range(RING)]
        self.bSGA = Buf("SGA", multi=True)
        self.bT1S = [Buf("T1S%d" % i, multi=True) for i in range(4)]
        self.bSGBS = [Buf("SGBS%d" % i, multi=True) for i in range(4)]
        self.bTSIN = [[self.bSGA], self.bT1S + self.bSGBS]
        self.bZBS = [Buf("ZBS%d" % i) for i in range(4)]
        self.bGN = [Buf("GN%d" % i) for i in range(3)]
        self.bHB = [Buf("HB%d" % i) for i in range(NHB)]
        self.pending_B2 = []
        self.bSG = [Buf("SG%d" % i) for i in range(2)]
        self.bJUNK = Buf("JUNK")
        self.bCONST = Buf("CONST", multi=True)
        self.bLCONST = Buf("LCONST", multi=True)
        self.bSS = Buf("SS")
        self.bRSTD = Buf("RSTD")
        self.bSSb = [Buf("SSb%d" % i) for i in range(16)]
        self.bRSb = [Buf("RSb%d" % i) for i in range(16)]
        self.bTMPF = [Buf("TMPF%d" % i) for i in range(2)]
        self.bYO = [Buf("YO%d" % i) for i in range(2)]
        self.bW = {}
        self.bDR = {k: Buf("DR_" + k, multi=True) for k in ("x1d", "t1d", "sgbd", "fd", "zs", "zpin", "zpout", "wbp0", "wbp1")}
        self.rr = {"hb": 0, "sg": 0, "t1s": 0, "sgbs": 0, "zbs": 0, "tmpf": 0, "evac": 0, "yo": 0}
        if not self.dry:
            C = S.chan
            self.ch_conv = {}
            self.ch_slot = [C("slot%d" % i) for i in range(RING)]
            self.ch_xin = [C("xin%d" % b) for b in range(NB)]
            self.ch_xout = [C("xout%d" % b) for b in range(NB)]
            self.ch_ft = C("ft")
            self.ch_ts = [C("ts%d" % i) for i in range(2)]
            self.ch_st = {k: [C("%s%d" % (k, i)) for i in range(4)] for k in ("t1s", "sgbs", "zbs")}
            self.ch_const = C("const")
            self.ch_gn = [C("gn%d" % i) for i in range(3)]
            self.ch_misc = C("misc")
            self.ch_cc = Chan(self.cc_sem)
            S.chans.append(self.ch_cc)
            self.ch_zt = C("zt")
            self.ch_fst = [C("fst%d" % i) for i in range(2)]
            self.ch_yo = [C("yo%d" % i) for i in range(2)]
            self.ch_wps = [C("wps%d" % i) for i in range(2)]

        self.prep_phase()
        if self.stop_after == "prep":
            return
        self.load_layer_consts(0)
        for g in range(NGRP):
            self.do_group(g, lc=None, la=0)
            if self.stop_after in ("g0", "g0ffn"):
                return
            if g == 3:
                self.all_gather()
        if self.stop_after in ("A0", "A0nocc"):
            return
        self.fourier_phase()
        if self.stop_after in ("F0", "F0nocc"):
            return
        self.load_layer_consts(1)
        self.phase_start(0, 1)
        for g in range(NGRP):
            self.do_group(g, lc=0, la=1)
            if g == 3:
                self.all_gather()
        self.fourier_phase()
        self.phase_start(1, None)
        for g in range(NGRP):
            self.do_group(g, lc=1, la=None)

    def bank(self):
        i = self.bank_i
        self.bank_i = (i + 1) % 6
        return self.psum[i], self.pbuf[i]

    def bank_bf(self):
        i = self.bankbf_i
        self.bankbf_i = (i + 1) % 2
        return self.psum_bf[i], self.pbbuf[i]

    def rot(self, key, n):
        i = self.rr[key]
        self.rr[key] = (i + 1) % n
        return i

    def mm_group(self, mms, reads, pbuf):
        nc = self.nc

        def fn():
            ins = None
            for (o, l, r, st, sp) in mms:
                ins = nc.tensor.matmul(o, l, r, start=st, stop=sp)
            return ins
        return self.S.op("pe", fn, reads=reads, writes=[pbuf])

    def piece(self, desc):
        i = self.piece_idx
        self.piece_idx += 1
        self.pieces.append((self.epoch, desc))
        if self.dry:
            return self.SLOT[i % RING], self.bSLOT[i % RING]
        while (self.next_load <= min(i + RING - 1, len(self.piece_list) - 1)
               and self.piece_list[self.next_load][0] == self.epoch):
            self._load_piece(self.next_load)
            self.next_load += 1
        assert self.next_load > i
        return self.SLOT[i % RING], self.bSLOT[i % RING]

    def _load_piece(self, p):
        nc = self.nc
        desc = self.piece_list[p][1]
        slot = p % RING
        sv = self.SLOT[slot]
        for (dst_fn, src, wkey) in desc:
            dst = dst_fn(sv)

            def fn(dst=dst, src=src):
                return nc.sync.dma_start(out=dst, in_=src)
            self.S.dma("sp", fn, self.ch_slot[slot], reads=[self.bW[wkey]], writes=[self.bSLOT[slot]])

    def prep_phase(self):
        S = self.S
        nc = self.nc
        self.next_load = 0
        self.epoch = 0
        self.wbp_done = False
        if self.dry:
            for l in range(L):
                for k in list(self.w_src.keys()) + ["wbp"]:
                    self.bW[(k, l)] = Buf("W")
            return
        def cdma(dst, src, chan, bufs):
            def fn():
                return nc.gpsimd.dma_start(out=dst, in_=src)
            S.dma("pool", fn, chan, writes=bufs)
        cdma(self.IDENT, self.ident_in, self.ch_const, [self.bCONST])
        cdma(self.ONES[0:1, :], self.ones_in, self.ch_const, [self.bCONST])
        cdma(self.D1P, self.d1p_in, self.ch_const, [self.bCONST])
        cdma(self.D1S, self.d1s_in, self.ch_const, [self.bCONST])
        def conv(k, l, cols=None, key=None):
            src = self.w_src[k][l]
            dst = self.w_bf[k][l]
            key = key if key is not None else (k, l)
            b = Buf("W_%s" % (key,), multi=True)
            self.bW[key] = b
            ch = S.chan("cv_%s_%s" % (k, "_".join(str(x) for x in key[1:])))
            if cols is not None:
                s_ = src[:, cols[0]:cols[1]].rearrange("(p a) c -> p a c", p=128)
                d_ = dst[:, cols[0]:cols[1]].rearrange("(p a) c -> p a c", p=128)

                def fn(s_=s_, d_=d_):
                    return nc.gpsimd.dma_start(out=d_, in_=s_)
                S.dma("pool", fn, ch, writes=[b])
                return
            rows = src.shape[0]
            nsplit = 2 if rows * src.shape[1] > 2 * 1024 * 1024 else 1
            rp = rows // nsplit
            for i in range(nsplit):
                s_ = src[i * rp:(i + 1) * rp, :].rearrange("(p a) c -> p (a c)", p=128)
                d_ = dst[i * rp:(i + 1) * rp, :].rearrange("(p a) c -> p (a c)", p=128)

                def fn(s_=s_, d_=d_):
                    return nc.gpsimd.dma_start(out=d_, in_=s_)
                S.dma("pool", fn, ch, writes=[b])
        self.conv = conv
        self.phase_start(None, 0)
        for blk, (c0, c1) in enumerate(FFN1_BLOCKS):
            conv("ffn1_g", 0, cols=(c0, c1), key=("ffn1_g", 0, blk))
            conv("ffn1_u", 0, cols=(c0, c1), key=("ffn1_u", 0, blk))
        conv("ffn1_d", 0)
        conv("win", 0)
        conv("wa", 0)
        self.pending_convs = [(k, 0) for k in ["wb", "wo", "ffn2_g", "ffn2_u", "ffn2_d"]] + \
            [(k, 1) for k in ["ffn1_g", "ffn1_u", "ffn1_d", "win", "wa", "wb", "wo", "ffn2_g", "ffn2_u", "ffn2_d"]]
        for l in range(L):
            self.bW[("wbp", l)] = Buf("W_wbp%d" % l, multi=True)

    def prep_wbp(self):
        S = self.S
        nc = self.nc
        bWBS = Buf("WBS")
        bCDT = Buf("CDT")
        bWPS = [Buf("WPS0"), Buf("WPS1")]

        def fc():
            return nc.gpsimd.dma_start(out=self.CDT.rearrange("p m t c -> p (m t c)"), in_=self.cdt_in)
        S.dma("pool", fc, self.ch_const, writes=[bCDT])
        for l in range(L):
            src = self.w_bf["wb"][l].rearrange("(c p) n -> p c n", p=128)

            def fn(src=src):
                return nc.sync.dma_start(out=self.WBS, in_=src)
            S.dma("sp", fn, self.ch_misc, reads=[self.bW[("wb", l)]], writes=[bWBS])
            for t in range(2):
                for g in range(4):
                    for cc in range(2):
                        wi = self.rot("tmpf", 2)
                        for dh in range(2):
                            ps, pb = self.bank()
                            mms = []
                            for mc in range(2):
                                mms.append((ps[:, :], self.CDT[:, mc, t, cc * 128:(cc + 1) * 128],
                                            self.WBS[:, g * 2 + mc, dh * 512:(dh + 1) * 512], mc == 0, mc == 1))
                            self.mm_group(mms, [bWBS, bCDT], pb)
                            dst = self.WPS[wi][:, dh * 512:(dh + 1) * 512]

                            def fe(dst=dst, ps=ps):
                                return nc.vector.tensor_copy(out=dst, in_=ps[:, :])
                            S.op("dve", fe, reads=[pb], writes=[bWPS[wi]])
                        row0 = t * D + g * 256 + cc * 128
                        ddst = self.wbp[l][row0:row0 + 128, :]

                        def fd(ddst=ddst, wi=wi):
                            return nc.sync.dma_start(out=ddst, in_=self.WPS[wi])
                        S.dma("sp", fd, self.ch_wps[wi], reads=[bWPS[wi]], writes=[self.bW[("wbp", l)]])

    def load_layer_consts(self, l):
        S = self.S
        nc = self.nc
        if self.dry:
            return
        for dst, src in ((self.WST, self.wsT[l]), (self.BSR[0:1, :], self.bsrow[l]), (self.GVB, self.gvb[l])):
            def fn(dst=dst, src=src):
                return nc.gpsimd.dma_start(out=dst, in_=src)
            S.dma("pool", fn, self.ch_const, writes=[self.bLCONST])

    def load_gain(self, slot, src):
        nc = self.nc

        def fn():
            return nc.gpsimd.dma_start(out=self.GN[slot], in_=src)
        self.S.dma("pool", fn, self.ch_gn[slot], writes=[self.bGN[slot]])

    def norm_A(self, b, col0=0):
        S = self.S
        nc = self.nc
        c = col0 + b
        ssb, rsb = self.bSSb[c], self.bRSb[c]

        def fz():
            return nc.vector.memset(self.SS[:, c:c + 1], 0.0)
        S.op("dve", fz, writes=[ssb])

        def f1():
            return nc.vector.scalar_tensor_tensor(out=self.JUNK, in0=self.X[b], scalar=1.0, in1=self.X[b],
                                                  op0=ALU.mult, op1=ALU.mult, accum_out=self.SS[:, c:c + 1])
        S.op("dve", f1, reads=[self.bX[b]], writes=[self.bJUNK, ssb])

        def f2():
            return nc.vector.tensor_scalar(out=self.RSTD[:, c:c + 1], in0=self.SS[:, c:c + 1], scalar1=1.0 / D,
                                           scalar2=EPS, op0=ALU.mult, op1=ALU.add)
        S.op("dve", f2, reads=[ssb], writes=[rsb])

        def f3a():
            return nc.scalar.activation(out=self.RSTD[:, c:c + 1], in_=self.RSTD[:, c:c + 1], func=AF.Sqrt)
        S.op("act", f3a, reads=[rsb], writes=[rsb])

    def norm_B1(self, b, gslot):
        S = self.S
        nc = self.nc
        rsb = self.bRSb[b]

        def f3():
            return nc.vector.reciprocal(out=self.RSTD[:, b:b + 1], in_=self.RSTD[:, b:b + 1])
        S.op("dve", f3, reads=[rsb], writes=[rsb])
        hi = self.rot("hb", NHB)

        def f4():
            return nc.vector.scalar_tensor_tensor(out=self.HB[hi], in0=self.X[b], scalar=self.RSTD[:, b:b + 1],
                                                  in1=self.GN[gslot], op0=ALU.mult, op1=ALU.mult)
        S.op("dve", f4, reads=[self.bX[b], rsb, self.bGN[gslot]], writes=[self.bHB[hi]])
        return hi

    def norm_B2(self, b, hi):
        self.transpose_block(self.HB[hi], self.bHB[hi], self.HT, self.bHT[b], b)

    def norm_to_HT(self, gslot):
        if self.dry:
            return
        his = {}
        for b in range(NB):
            self.norm_A(b)
            if b >= 1:
                his[b - 1] = self.norm_B1(b - 1, gslot)
            if b >= 2:
                self.norm_B2(b - 2, his[b - 2])
        his[NB - 1] = self.norm_B1(NB - 1, gslot)
        self.norm_B2(NB - 2, his[NB - 2])
        self.norm_B2(NB - 1, his[NB - 1])

    def make_final_tail(self, g):
        S = self.S
        nc = self.nc
        tok0 = g * G

        def final_B(b):
            rsb = self.bRSb[b]

            def f3():
                return nc.vector.reciprocal(out=self.RSTD[:, b:b + 1], in_=self.RSTD[:, b:b + 1])
            S.op("dve", f3, reads=[rsb], writes=[rsb])
            yi = self.rot("yo", 2)

            def f4():
                return nc.vector.scalar_tensor_tensor(out=self.YO[yi], in0=self.X[b], scalar=self.RSTD[:, b:b + 1],
                                                      in1=self.GN[0], op0=ALU.mult, op1=ALU.mult)
            S.op("dve", f4, reads=[self.bX[b], rsb, self.bGN[0]], writes=[self.bYO[yi]])
            dst = self.y_out[tok0 + b * 128: tok0 + (b + 1) * 128, :]

            def fs():
                return nc.gpsimd.dma_start(out=dst, in_=self.YO[yi])
            S.dma("pool", fs, self.ch_yo[yi], reads=[self.bYO[yi]], writes=[])

        def tail(b):
            if self.dry:
                return
            self.norm_A(b)
            if b >= 2:
                final_B(b - 2)
            if b == NB - 1:
                final_B(NB - 2)
                final_B(NB - 1)
        return tail

    def flush_pending(self):
        if self.dry:
            return
        for (b, hi) in self.pending_B2:
            self.norm_B2(b, hi)
        self.pending_B2 = []

    def make_tail(self, gslot, pre_fn=None):
        his = {}

        def tail(b):
            if self.dry:
                return
            if pre_fn is not None:
                pre_fn(b)
            self.norm_A(b)
            if b >= 2:
                his[b - 2] = self.norm_B1(b - 2, gslot)
            if b >= 4:
                self.norm_B2(b - 4, his[b - 4])
            if b == NB - 1:
                his[NB - 2] = self.norm_B1(NB - 2, gslot)
                his[NB - 1] = self.norm_B1(NB - 1, gslot)
                self.pending_B2 = [(bb, his[bb]) for bb in range(NB - 4, NB)]
        return tail

    def transpose_block(self, src, src_buf, dstT, dst_buf, b):
        S = self.S
        nc = self.nc
        ps, pb = self.bank_bf()

        def ft():
            ins = None
            for c in range(8):
                ins = nc.tensor.transpose(ps[:, c * 128:(c + 1) * 128], src[:, c * 128:(c + 1) * 128], self.IDENT)
            return ins
        S.op("pe", ft, reads=[src_buf, self.bCONST], writes=[pb])
        dst = dstT[:, :, b * 128:(b + 1) * 128]
        srcv = ps[:, :].rearrange("p (c t) -> p c t", c=8)

        def fe():
            return nc.scalar.copy(out=dst, in_=srcv)
        S.op("act", fe, reads=[pb], writes=[dst_buf])

    def ffn(self, pre, l, gslot, norm_done=False, tail=None):
        S = self.S
        nc = self.nc
        if not norm_done:
            self.norm_to_HT(gslot)
        wg, wu, wd = self.w_bf[pre + "_g"][l], self.w_bf[pre + "_u"][l], self.w_bf[pre + "_d"][l]
        for (c0, c1) in FFN_SPLITS:
            for pc in range(c0, c1, 2):
                col0 = pc * 128
                gsrc = wg[:, col0:col0 + 256].rearrange("(c p) n -> p c n", p=128)
                usrc = wu[:, col0:col0 + 256].rearrange("(c p) n -> p c n", p=128)
                if pre == "ffn1" and l == 0:
                    blk = [i for i, (a0, a1) in enumerate(FFN1_BLOCKS) if a0 <= col0 < a1][0]
                    kg, ku = (pre + "_g", l, blk), (pre + "_u", l, blk)
                else:
                    kg, ku = (pre + "_g", l), (pre + "_u", l)
                desc = [
                    (lambda sv: sv[:, 0:2048].rearrange("p (c n) -> p c n", c=8), gsrc, kg),
                    (lambda sv: sv[:, 2048:4096].rearrange("p (c n) -> p c n", c=8), usrc, ku),
                ]
                sv, sb = self.piece(desc)
                if self.dry:
                    continue
                gw = sv[:, 0:2048].rearrange("p (c n) -> p c n", c=8)
                uw = sv[:, 2048:4096].rearrange("p (c n) -> p c n", c=8)
                for st in range(2):
                    if st == 1:
                        self.flush_pending()
                    for fc in range(2):
                        ch = pc + fc - c0
                        tsl = slice(st * 512, (st + 1) * 512)
                        hbufs = self.bHT[st * 4:(st + 1) * 4]
                        psg, pbg = self.bank()
                        self.mm_group([(psg[:, :], gw[:, kc, fc * 128:(fc + 1) * 128], self.HT[:, kc, tsl], kc == 0, kc == 7)
                                       for kc in range(8)], [sb] + hbufs, pbg)
                        psu, pbu = self.bank()
                        self.mm_group([(psu[:, :], uw[:, kc, fc * 128:(fc + 1) * 128], self.HT[:, kc, tsl], kc == 0, kc == 7)
                                       for kc in range(8)], [sb] + hbufs, pbu)
                        si = self.rot("sg", 2)

                        def fa(si=si, psg=psg):
                            return nc.scalar.activation(out=self.SG[si], in_=psg[:, :], func=AF.Silu)
                        S.op("act", fa, reads=[pbg], writes=[self.bSG[si]])
                        dst = self.ACTT[:, ch, tsl]

                        def fm(dst=dst, si=si, psu=psu):
                            return nc.vector.tensor_tensor(out=dst, in0=self.SG[si], in1=psu[:, :], op=ALU.mult)
                        S.op("dve", fm, reads=[self.bSG[si], pbu], writes=[self.bUT[ch][st]])
            nch = c1 - c0
            for dh in range(2):
                dsrc = wd[c0 * 128:c1 * 128, dh * 512:(dh + 1) * 512].rearrange("(c p) n -> p c n", p=128)
                desc = [(lambda sv, nch=nch: sv[:, 0:nch * 512].rearrange("p (c n) -> p c n", c=nch), dsrc, (pre + "_d", l))]
                sv, sb = self.piece(desc)
                if self.dry:
                    continue
                dw = sv[:, 0:nch * 512].rearrange("p (c n) -> p c n", c=nch)
                for b in range(NB):
                    ps, pb = self.bank()
                    self.mm_group([(ps[:, :], self.ACTT[:, c, b * 128:(b + 1) * 128], dw[:, c, :], c == 0, c == nch - 1)
                                   for c in range(nch)], [sb] + [self.bUT[c][b // 4] for c in range(nch)], pb)
                    xs = self.X[b][:, dh * 512:(dh + 1) * 512]

                    def fr(xs=xs, ps=ps):
                        return nc.vector.scalar_tensor_tensor(out=xs, in0=ps[:, :], scalar=0.5, in1=xs,
                                                              op0=ALU.mult, op1=ALU.add)
                    S.op("dve", fr, reads=[pb, self.bX[b]], writes=[self.bX[b]])
                    if tail is not None and dh == 1 and c1 == FFN_SPLITS[-1][1]:
                        tail(b)

    def mixer_in(self, l, g, norm_done=False):
        S = self.S
        nc = self.nc
        if not norm_done:
            self.norm_to_HT(1)
        win = self.w_bf["win"][l]
        tok0 = g * G

        def wpiece(col0):
            src = win[:, col0:col0 + 512].rearrange("(c p) n -> p c n", p=128)
            desc = [(lambda sv: sv[:, 0:4096].rearrange("p (c n) -> p c n", c=8), src, ("win", l))]
            sv, sb = self.piece(desc)
            return sv[:, 0:4096].rearrange("p (c n) -> p c n", c=8), sb

        if not self.dry:
            def fz():
                return nc.vector.memset(self.SS[:, 8:16], 0.0)
            S.op("dve", fz, writes=[self.bSS])
        for i in range(2):
            w, sb = wpiece(D + i * 512)
            if self.dry:
                continue
            for b in range(NB):
                if b == 4:
                    self.flush_pending()
                ps, pb = self.bank()
                self.mm_group([(ps[:, :], self.HT[:, kc, b * 128:(b + 1) * 128], w[:, kc, :], kc == 0, kc == 7)
                               for kc in range(8)], [sb, self.bHT[b]], pb)
                dst = self.VN[b][:, i * 512:(i + 1) * 512]

                def fa(dst=dst, ps=ps):
                    return nc.scalar.activation(out=dst, in_=ps[:, :], func=AF.Gelu)
                S.op("act", fa, reads=[pb], writes=[self.bVN[b]])
        if not self.dry:
            for b in range(NB):
                def f1(b=b):
                    return nc.vector.scalar_tensor_tensor(out=self.JUNK, in0=self.VN[b], scalar=1.0, in1=self.VN[b],
                                                          op0=ALU.mult, op1=ALU.mult, accum_out=self.SS[:, 8 + b:9 + b])
                S.op("dve", f1, reads=[self.bVN[b]], writes=[self.bJUNK, self.bSS])
            def f2():
                return nc.vector.tensor_scalar(out=self.RSTD[:, 8:16], in0=self.SS[:, 8:16], scalar1=1.0 / D, scalar2=EPS,
                                               op0=ALU.mult, op1=ALU.add)
            S.op("dve", f2, reads=[self.bSS], writes=[self.bRSTD])
            def f3a():
                return nc.scalar.activation(out=self.RSTD[:, 8:16], in_=self.RSTD[:, 8:16], func=AF.Sqrt)
            S.op("act", f3a, reads=[self.bRSTD], writes=[self.bRSTD])
            def f3():
                return nc.vector.reciprocal(out=self.RSTD[:, 8:16], in_=self.RSTD[:, 8:16])
            S.op("dve", f3, reads=[self.bRSTD], writes=[self.bRSTD])
            for b in range(NB):
                def f4(b=b):
                    return nc.vector.scalar_tensor_tensor(out=self.VN[b], in0=self.VN[b], scalar=self.RSTD[:, 8 + b:9 + b],
                                                          in1=self.GVB, op0=ALU.mult, op1=ALU.mult)
                S.op("dve", f4, reads=[self.bVN[b], self.bRSTD, self.bLCONST], writes=[self.bVN[b]])
        for i in range(2):
            w, sb = wpiece(i * 512)
            if self.dry:
                continue
            for st in range(2):
                tsl = slice(st * 512, (st + 1) * 512)
                for cc in range(4):
                    c = i * 4 + cc
                    ps, pb = self.bank()
                    self.mm_group([(ps[:, :], w[:, kc, cc * 128:(cc + 1) * 128], self.HT[:, kc, tsl], kc == 0, kc == 7)
                                   for kc in range(8)], [sb] + self.bHT[st * 4:(st + 1) * 4], pb)
                    dst = self.UT[:, c, tsl]

                    def fa(dst=dst, ps=ps):
                        return nc.scalar.activation(out=dst, in_=ps[:, :], func=AF.Gelu)
                    S.op("act", fa, reads=[pb], writes=[self.bUT[c][st]])
        if not self.dry:
            for st in range(2):
                tsl = slice(st * 512, (st + 1) * 512)
                for h in range(8):
                    ps, pb = self.bank()
                    mms = []
                    for bb in range(4):
                        b = st * 4 + bb
                        o = ps[:, bb * 128:(bb + 1) * 128]
                        mms.append((o, self.ONES[0:1, :], self.BSR[0:1, h * 128:(h + 1) * 128], True, False))
                        mms.append((o, self.VN[b][:, h * 128:(h + 1) * 128], self.WST[:, h * 128:(h + 1) * 128], False, True))
                    self.mm_group(mms, [self.bCONST, self.bLCONST] + self.bVN[st * 4:(st + 1) * 4], pb)
                    dst = self.UT[:, h, tsl]

                    def fs(dst=dst, ps=ps, h=h):
                        return nc.vector.tensor_tensor(out=dst, in0=ps[:, :], in1=dst, op=ALU.mult)
                    S.op("dve", fs, reads=[pb, self.bUT[h][st]], writes=[self.bUT[h][st]])
        wa = self.w_bf["wa"][l]
        for i in range(2):
            w, sb = wpiece(3 * D + i * 512)
            if not self.dry:
                for st in range(2):
                    tsl = slice(st * 512, (st + 1) * 512)
                    for cc in range(4):
                        ps, pb = self.bank()
                        self.mm_group([(ps[:, :], w[:, kc, cc * 128:(cc + 1) * 128], self.HT[:, kc, tsl], kc == 0, kc == 7)
                                       for kc in range(8)], [sb] + self.bHT[st * 4:(st + 1) * 4], pb)
                        dst = self.SGA[:, cc, tsl]

                        def fa(dst=dst, ps=ps):
                            return nc.scalar.activation(out=dst, in_=ps[:, :], func=AF.Sigmoid)
                        S.op("act", fa, reads=[pb], writes=[self.bSGA])
            src = wa[:, i * 512:(i + 1) * 512].rearrange("(c p) n -> p c n", p=128)
            desc = [(lambda sv: sv[:, 0:4096].rearrange("p (c n) -> p c n", c=8), src, ("wa", l))]
            sv, sb2 = self.piece(desc)
            if self.dry:
                continue
            w2 = sv[:, 0:4096].rearrange("p (c n) -> p c n", c=8)
            for st in range(2):
                tsl = slice(st * 512, (st + 1) * 512)
                for cc in range(4):
                    c = i * 4 + cc
                    ps, pb = self.bank()
                    self.mm_group([(ps[:, :], w2[:, kc, cc * 128:(cc + 1) * 128], self.UT[:, kc, tsl], kc == 0, kc == 7)
                                   for kc in range(8)], [sb2] + [self.bUT[kc][st] for kc in range(8)], pb)
                    ti = self.rot("t1s", 4)
                    dstt = self.T1S[ti]

                    def fm(dstt=dstt, ps=ps, cc=cc, tsl=tsl):
                        return nc.vector.tensor_tensor(out=dstt, in0=ps[:, :], in1=self.SGA[:, cc, tsl], op=ALU.mult)
                    S.op("dve", fm, reads=[pb, self.bSGA], writes=[self.bT1S[ti]])
                    dd = self.t1d[c, :, tok0 + st * 512: tok0 + (st + 1) * 512]

                    def fd(dd=dd, dstt=dstt):
                        return nc.gpsimd.dma_start(out=dd, in_=dstt)
                    S.dma("pool", fd, self.ch_st["t1s"][ti], reads=[self.bT1S[ti]], writes=[self.bDR["t1d"]])
        for i in range(2):
            w, sb = wpiece(2 * D + i * 512)
            if self.dry:
                continue
            for b in range(NB):
                ps, pb = self.bank()
                self.mm_group([(ps[:, :], self.HT[:, kc, b * 128:(b + 1) * 128], w[:, kc, :], kc == 0, kc == 7)
                               for kc in range(8)], [sb, self.bHT[b]], pb)
                zi = self.rot("zbs", 4)
                dstz = self.ZBS[zi]

                def fe(dstz=dstz, ps=ps):
                    return nc.scalar.copy(out=dstz, in_=ps[:, :])
                S.op("act", fe, reads=[pb], writes=[self.bZBS[zi]])
                t0 = tok0 + b * 128
                if t0 < 2 * SP_TOK:
                    seq = t0 // SP_TOK
                    tl = t0 % SP_TOK
                    base = self.zp_in.rearrange("(c s t) e -> c s t e", c=8, s=2)
                    dd = base[i * 4:(i + 1) * 4, seq, tl:tl + 128, :].rearrange("c t e -> t c e")
                    dbuf = self.bDR["zpin"]
                else:
                    tl = t0 - 2 * SP_TOK
                    dd = self.zs[i * 4:(i + 1) * 4, tl:tl + 128, :].rearrange("c t e -> t c e")
                    dbuf = self.bDR["zs"]
                sv_ = dstz.rearrange("p (c e) -> p c e", c=4)

                def fd(dd=dd, sv_=sv_):
                    return nc.gpsimd.dma_start(out=dd, in_=sv_)
                S.dma("pool", fd, self.ch_st["zbs"][zi], reads=[self.bZBS[zi]], writes=[dbuf])
        for i in range(2):
            w, sb = wpiece(4 * D + i * 512)
            if self.dry:
                continue
            for st in range(2):
                tsl = slice(st * 512, (st + 1) * 512)
                for cc in range(4):
                    c = i * 4 + cc
                    ps, pb = self.bank()
                    self.mm_group([(ps[:, :], w[:, kc, cc * 128:(cc + 1) * 128], self.HT[:, kc, tsl], kc == 0, kc == 7)
                                   for kc in range(8)], [sb] + self.bHT[st * 4:(st + 1) * 4], pb)
                    gi = self.rot("sgbs", 4)
                    dstg = self.SGBS[gi]

                    def fa(dstg=dstg, ps=ps):
                        return nc.scalar.activation(out=dstg, in_=ps[:, :], func=AF.Sigmoid)
                    S.op("act", fa, reads=[pb], writes=[self.bSGBS[gi]])
                    dd = self.sgbd[c, :, tok0 + st * 512: tok0 + (st + 1) * 512]

                    def fd(dd=dd, dstg=dstg):
                        return nc.gpsimd.dma_start(out=dd, in_=dstg)
                    S.dma("pool", fd, self.ch_st["sgbs"][gi], reads=[self.bSGBS[gi]], writes=[self.bDR["sgbd"]])

    def mixer_out(self, l, g, tail=None):
        S = self.S
        nc = self.nc
        tok0 = g * G
        wbp = self.wbp[l]
        wo = self.w_bf["wo"][l]
        def load_ts(p):
            ti = p % 2
            for a, dr, key in ((0, self.t1d, "t1d"), (1, self.sgbd, "sgbd")):
                src = dr[2 * p:2 * p + 2, :, tok0:tok0 + G].rearrange("c p t -> p c t")
                dst = self.TSIN[ti][:, a]

                def fn(src=src, dst=dst):
                    return nc.gpsimd.dma_start(out=dst, in_=src)
                S.dma("pool", fn, self.ch_ts[ti], reads=[self.bDR[key]], writes=self.bTSIN[ti])
        if not self.dry:
            load_ts(0)
            load_ts(1)
        for p in range(4):
            src = wbp[:, p * 256:(p + 1) * 256].rearrange("(c p) n -> p c n", p=128)
            desc = [(lambda sv: sv[:, 0:4096].rearrange("p (c n) -> p c n", c=16), src, ("wbp", l))]
            sv, sb = self.piece(desc)
            if self.dry:
                continue
            w = sv[:, 0:4096].rearrange("p (c n) -> p c n", c=16)
            ti = p % 2
            for st in range(2):
                tsl = slice(st * 512, (st + 1) * 512)
                for dc in range(2):
                    c = 2 * p + dc
                    ps, pb = self.bank()
                    self.mm_group([(ps[:, :], w[:, kc, dc * 128:(dc + 1) * 128], self.FT[:, kc, tsl], kc == 0, kc == 15)
                                   for kc in range(16)], [sb, self.bFT], pb)
                    fi = self.rot("tmpf", 2)
                    tmp = self.TMPF[fi]

                    def f1(tmp=tmp, ps=ps, ti=ti, dc=dc, tsl=tsl):
                        return nc.vector.tensor_tensor(out=tmp, in0=ps[:, :], in1=self.TSIN[ti][:, 1, dc, tsl], op=ALU.mult)
                    S.op("dve", f1, reads=[pb] + self.bTSIN[ti], writes=[self.bTMPF[fi]])
                    dst = self.MT[:, c, tsl]

                    def f2(dst=dst, tmp=tmp, ti=ti, dc=dc, tsl=tsl):
                        return nc.gpsimd.tensor_tensor(out=dst, in0=tmp, in1=self.TSIN[ti][:, 0, dc, tsl], op=ALU.add)
                    S.op("pool", f2, reads=[self.bTMPF[fi]] + self.bTSIN[ti], writes=self.bHT[st * 4:(st + 1) * 4])
            if p + 2 < 4:
                load_ts(p + 2)
        for dh in range(2):
            src = wo[:, dh * 512:(dh + 1) * 512].rearrange("(c p) n -> p c n", p=128)
            desc = [(lambda sv: sv[:, 0:4096].rearrange("p (c n) -> p c n", c=8), src, ("wo", l))]
            sv, sb = self.piece(desc)
            if self.dry:
                continue
            w = sv[:, 0:4096].rearrange("p (c n) -> p c n", c=8)
            for b in range(NB):
                ps, pb = self.bank()
                self.mm_group([(ps[:, :], self.MT[:, kc, b * 128:(b + 1) * 128], w[:, kc, :], kc == 0, kc == 7)
                               for kc in range(8)], [sb, self.bHT[b]], pb)
                xs = self.X[b][:, dh * 512:(dh + 1) * 512]

                def fr(xs=xs, ps=ps):
                    return nc.vector.tensor_tensor(out=xs, in0=ps[:, :], in1=xs, op=ALU.add)
                S.op("dve", fr, reads=[pb, self.bX[b]], writes=[self.bX[b]])
                if tail is not None and dh == 1:
                    tail(b)

    def load_x(self, g, lc):
        S = self.S
        nc = self.nc
        if self.dry or g >= NGRP:
            return
        tok0 = g * G
        src_t = self.x_in if lc is None else self.x1d
        for b in range(NB):
            src = src_t[tok0 + b * 128: tok0 + (b + 1) * 128, :]

            def fl(src=src, b=b):
                return nc.gpsimd.dma_start(out=self.X[b], in_=src)
            rd = [] if lc is None else [self.bDR["x1d"]]
            S.dma("pool", fl, self.ch_xin[b], reads=rd, writes=[self.bX[b]])

    def load_ft(self, g):
        S = self.S
        nc = self.nc
        if self.dry or g >= NGRP:
            return
        tok0 = g * G
        for a in range(2):
            src = self.fd[a, :, :, tok0:tok0 + G].rearrange("c p t -> p c t")
            dstf = self.FT[:, a * 8:(a + 1) * 8, :]

            def ff(src=src, dstf=dstf):
                return nc.gpsimd.dma_start(out=dstf, in_=src)
            S.dma("pool", ff, self.ch_ft, reads=[self.bDR["fd"]], writes=[self.bFT])

    def phase_start(self, lc, la):
        if self.dry:
            return
        if lc is not None:
            self.load_gain(2, self.gains[lc, 2])
        if la is not None:
            self.load_gain(0, self.gains[la, 0])
            self.load_gain(1, self.gains[la, 1])
        else:
            self.load_gain(0, self.gfinal)
        self.load_x(0, lc)
        if lc is not None:
            self.load_ft(0)

    def do_group(self, g, lc, la):
        S = self.S
        nc = self.nc
        tok0 = g * G

        def store_x1(b):
            dst = self.x1d[tok0 + b * 128: tok0 + (b + 1) * 128, :]

            def fs():
                return nc.gpsimd.dma_start(out=dst, in_=self.X[b])
            S.dma("pool", fs, self.ch_xout[b], reads=[self.bX[b]], writes=[self.bDR["x1d"]])

        if lc is not None:
            self.mixer_out(lc, g, tail=self.make_tail(2))
            self.load_ft(g + 1)
            if la is not None:
                self.ffn("ffn2", lc, 2, norm_done=True, tail=self.make_tail(0))
            else:
                self.ffn("ffn2", lc, 2, norm_done=True, tail=self.make_final_tail(g))
        if la is not None:
            self.ffn("ffn1", la, 0, norm_done=(lc is not None), tail=self.make_tail(1, pre_fn=store_x1))
            self.load_x(g + 1, lc)
            if lc is None:
                self.issue_convs(1)
            if self.stop_after != "g0ffn":
                self.mixer_in(la, g, norm_done=True)
            if lc is None:
                self.issue_convs(1 if g < NGRP - 1 else 99)
        else:
            self.load_x(g + 1, lc)

    def issue_convs(self, n):
        if self.dry:
            return
        for _ in range(n):
            if not self.pending_convs:
                return
            k, l = self.pending_convs.pop(0)
            self.conv(k, l)

    def final_norm(self, g):
        S = self.S
        nc = self.nc
        if self.dry:
            return
        tok0 = g * G
        def fz():
            return nc.vector.memset(self.SS, 0.0)
        S.op("dve", fz, writes=[self.bSS])
        for b in range(NB):
            def f1(b=b):
                return nc.vector.scalar_tensor_tensor(out=self.JUNK, in0=self.X[b], scalar=1.0, in1=self.X[b],
                                                      op0=ALU.mult, op1=ALU.mult, accum_out=self.SS[:, b:b + 1])
            S.op("dve", f1, reads=[self.bX[b]], writes=[self.bJUNK, self.bSS])
        def f2():
            return nc.vector.tensor_scalar(out=self.RSTD[:, 0:NB], in0=self.SS[:, 0:NB], scalar1=1.0 / D, scalar2=EPS,
                                           op0=ALU.mult, op1=ALU.add)
        S.op("dve", f2, reads=[self.bSS], writes=[self.bRSTD])
        def f3a():
            return nc.scalar.activation(out=self.RSTD[:, 0:NB], in_=self.RSTD[:, 0:NB], func=AF.Sqrt)
        S.op("act", f3a, reads=[self.bRSTD], writes=[self.bRSTD])
        def f3():
            return nc.vector.reciprocal(out=self.RSTD[:, 0:NB], in_=self.RSTD[:, 0:NB])
        S.op("dve", f3, reads=[self.bRSTD], writes=[self.bRSTD])
        for b in range(NB):
            yi = self.rot("yo", 2)
            def f4(b=b, yi=yi):
                return nc.vector.scalar_tensor_tensor(out=self.YO[yi], in0=self.X[b], scalar=self.RSTD[:, b:b + 1],
                                                      in1=self.GN[0], op0=ALU.mult, op1=ALU.mult)
            S.op("dve", f4, reads=[self.bX[b], self.bRSTD, self.bGN[0]], writes=[self.bYO[yi]])
            dst = self.y_out[tok0 + b * 128: tok0 + (b + 1) * 128, :]

            def fs(dst=dst, yi=yi):
                return nc.gpsimd.dma_start(out=dst, in_=self.YO[yi])
            S.dma("pool", fs, self.ch_yo[yi], reads=[self.bYO[yi]], writes=[])

    def all_gather(self):
        S = self.S
        nc = self.nc
        if self.dry:
            return

        def fn():
            return nc.gpsimd.collective_compute("AllGather", ALU.bypass, replica_groups=[list(range(NCORE))],
                                                ins=[self.zp_in_t.ap().opt()], outs=[self.zp_out_t.ap().opt()])
        S.dma("pool", fn, self.ch_cc, reads=[self.bDR["zpin"]], writes=[self.bDR["zpout"]], inc=1)

    def fourier_phase(self):
        S = self.S
        nc = self.nc
        self.epoch += 1
        if self.dry:
            return
        S.barrier()
        if not self.wbp_done:
            self.prep_wbp()
            self.wbp_done = True
            S.barrier()
        bTab = Buf("FTAB", multi=True)
        bZT = Buf("ZT", multi=True)
        bY = [Buf("Y%d" % i) for i in range(64)]
        bFST = [Buf("FST0", multi=True), Buf("FST1", multi=True)]
        for dst, src in ((self.TPT.rearrange("p j c -> p (j c)"), self.tp_in), (self.TST.rearrange("p j c -> p (j c)"), self.ts_in)):
            def fn(dst=dst, src=src):
                return nc.gpsimd.dma_start(out=dst, in_=src)
            S.dma("pool", fn, self.ch_const, writes=[bTab])
        units = [("s", h, None) for h in range(2)] + [("p", seq, ch) for seq in range(2) for ch in range(8)]
        fst_i = 0
        zp = self.zp_out.rearrange("(r c s t) e -> r c s t e", r=NCORE, c=8, s=2)
        for (kind, a, bch) in units:
            if kind == "p":
                seq, ch = a, bch
                for r in range(NCORE):
                    src = zp[r, ch, seq].rearrange("(th tl) e -> th tl e", th=16)
                    dst = self.ZT[r * 16:(r + 1) * 16]

                    def fn(src=src, dst=dst):
                        return nc.sync.dma_start(out=dst, in_=src)
                    S.dma("sp", fn, self.ch_zt, reads=[self.bDR["zpout"]], writes=[bZT])
                d1 = self.D1P
            else:
                h = a
                for q in range(4):
                    src = self.zs[2 * q + h].rearrange("(th tl) e -> th tl e", th=32)
                    dst = self.ZT[q * 32:(q + 1) * 32]

                    def fn(src=src, dst=dst):
                        return nc.sync.dma_start(out=dst, in_=src)
                    S.dma("sp", fn, self.ch_zt, reads=[self.bDR["zs"]], writes=[bZT])
                d1 = self.D1S
            for cp in range(64):
                ps, pb = self.bank()
                mms = []
                for k in range(2):
                    c = 2 * cp + k
                    mms.append((ps[:, k * 256:(k + 1) * 256], self.ZT[:, :, c], d1, True, True))
                self.mm_group(mms, [bZT, self.bCONST], pb)
                dst = self.Y[:, 2 * cp:2 * cp + 2, :]
                srcv = ps[:, :].rearrange("p (c j) -> p c j", c=2)
                if cp % 2 == 0:
                    def fe(dst=dst, srcv=srcv):
                        return nc.scalar.copy(out=dst, in_=srcv)
                    S.op("act", fe, reads=[pb], writes=[bY[cp]])
                else:
                    def fe(dst=dst, srcv=srcv):
                        return nc.vector.tensor_copy(out=dst, in_=srcv)
                    S.op("dve", fe, reads=[pb], writes=[bY[cp]])
            if kind == "p":
                fi = fst_i
                fst_i ^= 1
                fst = self.FST[fi][:, 0:2 * SP_TOK].rearrange("p (a t) -> p a t", a=2)
                for jb in range(8):
                    ps, pb = self.bank()
                    mms = []
                    for jj in range(16):
                        j = jb * 16 + jj
                        o = ps[:, jj * 32:(jj + 1) * 32]
                        mms.append((o, self.Y[:, :, j], self.TPT[:, j, 16:48], True, False))
                        mms.append((o, self.Y[:, :, 128 + j], self.TPT[:, j, 0:32], False, True))
                    self.mm_group(mms, bY + [bTab], pb)
                    srcv = ps[:, :].rearrange("p (j a r) -> p j a r", j=16, a=2)
                    dstv = fst.rearrange("p a (r j) -> p j a r", j=128)[:, jb * 16:(jb + 1) * 16]
                    eng = self.rot("evac", 2)
                    if eng == 0:
                        def fe(dstv=dstv, srcv=srcv):
                            return nc.scalar.copy(out=dstv, in_=srcv)
                        S.op("act", fe, reads=[pb], writes=[bFST[fi]])
                    else:
                        def fe(dstv=dstv, srcv=srcv):
                            return nc.vector.tensor_copy(out=dstv, in_=srcv)
                        S.op("dve", fe, reads=[pb], writes=[bFST[fi]])
                dd = self.fd[:, ch, :, seq * SP_TOK:(seq + 1) * SP_TOK].rearrange("a p t -> p a t")

                def fd_(dd=dd, fst=fst):
                    return nc.gpsimd.dma_start(out=dd, in_=fst)
                S.dma("pool", fd_, self.ch_fst[fi], reads=[bFST[fi]], writes=[self.bDR["fd"]])
            else:
                for q in range(4):
                    fi = fst_i
                    fst_i ^= 1
                    fst = self.FST[fi].rearrange("p (a t) -> p a t", a=2)
                    for jp in range(16):
                        ps, pb = self.bank()
                        mms = []
                        for jj in range(2):
                            kl = jp * 2 + jj
                            col = q * 32 + kl
                            o = ps[:, jj * 256:(jj + 1) * 256]
                            mms.append((o, self.Y[:, :, col], self.TST[:, kl, 128:384], True, False))
                            mms.append((o, self.Y[:, :, 128 + col], self.TST[:, kl, 0:256], False, True))
                        self.mm_group(mms, bY + [bTab], pb)
                        srcv = ps[:, :].rearrange("p (j a r) -> p j a r", j=2, a=2)
                        dstv = fst.rearrange("p a (r j) -> p j a r", j=32)[:, jp * 2:(jp + 1) * 2]
                        eng = self.rot("evac", 2)
                        if eng == 0:
                            def fe(dstv=dstv, srcv=srcv):
                                return nc.scalar.copy(out=dstv, in_=srcv)
                            S.op("act", fe, reads=[pb], writes=[bFST[fi]])
                        else:
                            def fe(dstv=dstv, srcv=srcv):
                                return nc.vector.tensor_copy(out=dstv, in_=srcv)
                            S.op("dve", fe, reads=[pb], writes=[bFST[fi]])
                    dd = self.fd[:, 2 * q + h, :, 2 * SP_TOK:2 * SP_TOK + S_SAMPLE].rearrange("a p t -> p a t")

                    def fd_(dd=dd, fst=fst):
                        return nc.gpsimd.dma_start(out=dd, in_=fst)
                    S.dma("pool", fd_, self.ch_fst[fi], reads=[bFST[fi]], writes=[self.bDR["fd"]])
        S.barrier()


_NC_CACHE = {}


def _get_nc(debug=False, stop_after=None):
    key = (debug, stop_after)
    if key not in _NC_CACHE:
        _NC_CACHE[key] = Builder(debug=debug, stop_after=stop_after).build()
    return _NC_CACHE[key]


def _in_maps(x_prompt, x_sample, ffn1_norm, ffn1_w_gate, ffn1_w_up, ffn1_w_down, mix_norm, w_in, sgu_norm, sgu_w,
             sgu_b, w_branch_a, w_branch_b, w_out, ffn2_norm, ffn2_w_gate, ffn2_w_up, ffn2_w_down, final_norm):
    f = np.float32
    A = lambda a: np.ascontiguousarray(np.asarray(a, dtype=f))
    x_prompt, x_sample = A(x_prompt), A(x_sample)
    gains = np.stack([A(ffn1_norm), A(mix_norm), A(ffn2_norm)], axis=1)
    gains = np.ascontiguousarray(np.broadcast_to(gains[:, :, None, :], (L, 3, 128, D)))
    gfinal = np.ascontiguousarray(np.broadcast_to(A(final_norm)[None, :], (128, D)))
    gvb = np.ascontiguousarray(np.broadcast_to(A(sgu_norm)[:, None, :], (L, 128, D)))
    wsT = np.ascontiguousarray(A(sgu_w).transpose(0, 3, 1, 2).reshape(L, 128, D))
    bsrow = np.ascontiguousarray(A(sgu_b).reshape(L, 1, D))
    shared = dict(
        ffn1_w_gate=A(ffn1_w_gate), ffn1_w_up=A(ffn1_w_up), ffn1_w_down=A(ffn1_w_down),
        ffn2_w_gate=A(ffn2_w_gate), ffn2_w_up=A(ffn2_w_up), ffn2_w_down=A(ffn2_w_down),
        w_in=A(w_in), w_branch_a=A(w_branch_a), w_branch_b=A(w_branch_b), w_out=A(w_out),
        gains=gains, gfinal=gfinal, gvb=gvb, wsT=wsT, bsrow=bsrow,
        ident=np.eye(128, dtype=f), ones=np.ones((1, 128), dtype=f),
    )
    maps = []
    for c in range(NCORE):
        xs = np.concatenate([x_prompt[0, SP_TOK * c:SP_TOK * (c + 1)], x_prompt[1, SP_TOK * c:SP_TOK * (c + 1)],
                             x_sample[c]], axis=0)
        m = dict(shared)
        m["x"] = np.ascontiguousarray(xs)
        m.update(_tables(c))
        maps.append(m)
    return maps


def kernel(**inputs):
    nc = _get_nc()
    maps = _in_maps(**inputs)
    res = run_bass_kernel_spmd(nc, maps, core_ids=list(range(NCORE)))
    yp = np.empty((2, S_PROMPT, D), np.float32)
    ys = np.empty((NCORE, S_SAMPLE, D), np.float32)
    for c in range(NCORE):
        y = np.asarray(res.results[c]["y"], dtype=np.float32)
        yp[0, SP_TOK * c:SP_TOK * (c + 1)] = y[0:SP_TOK]
        yp[1, SP_TOK * c:SP_TOK * (c + 1)] = y[SP_TOK:2 * SP_TOK]
        ys[c] = y[2 * SP_TOK:]
    return (yp, ys)
```
